# Optimizing a Trainium2 kernel written in Bass

```python
import jax, jax.numpy as jnp
from jax import lax
import numpy as np

D_MODEL = 2048
BATCH = 4
SEQ = 2048
DEPTH = 1

D_MIX = D_MODEL
D_POOL = D_MIX // 2
POOL_WINDOWS = (2, 4, 8, 16)
N_POOL_GROUPS = len(POOL_WINDOWS)
POOL_GROUP_DIM = D_POOL // N_POOL_GROUPS
D_GLA = D_MIX - D_POOL
GLA_HEADS = 4
GLA_DV = D_GLA // GLA_HEADS
GLA_DK_TOTAL = D_GLA // 2
GLA_DK = GLA_DK_TOTAL // GLA_HEADS
GLA_GATE_RANK = 16
GATE_LOGIT_NORMALIZER = 16.0
CHUNK = 64
D_IN = D_POOL + 2 * GLA_DK_TOTAL + 2 * D_GLA + GLA_GATE_RANK
D_FF = 5632
EPS = 1e-6

kernel_name = "hymba_pool_gla_macaron_block"


def rmsnorm(x, g):
    xf = x.astype(jnp.float32)
    y = xf * lax.rsqrt(jnp.mean(xf * xf, axis=-1, keepdims=True) + EPS)
    return (y * g.astype(jnp.float32)).astype(x.dtype)


def swiglu(h, w_in, w_out):
    gu = h @ w_in
    gate, up = gu[..., :D_FF], gu[..., D_FF:]
    return (jax.nn.silu(gate) * up) @ w_out


def pool_mixer(u, w_pool, pool_scale):
    B, S, _ = u.shape
    uf = u.astype(jnp.float32).reshape(B, S, N_POOL_GROUPS, POOL_GROUP_DIM)
    cs = jnp.cumsum(uf, axis=1)
    pos1 = jnp.arange(1, S + 1, dtype=jnp.int32)
    means = []
    for gi, w in enumerate(POOL_WINDOWS):
        c = cs[:, :, gi]
        shifted = jnp.pad(c, ((0, 0), (w, 0), (0, 0)))[:, :S]
        cnt = jnp.minimum(pos1, w).astype(jnp.float32)[None, :, None]
        means.append((c - shifted) / cnt)
    pooled = jnp.stack(means, axis=2) - uf
    y = jnp.einsum('bsgc,gcd->bsgd', pooled.astype(u.dtype), w_pool)
    return y.reshape(B, S, D_POOL) * pool_scale


def gla_mixer(q, k, v, g_out, gate_lr, w_alpha, b_alpha, gla_norm):
    B, S, _ = q.shape
    N = S // CHUNK
    log_alpha = jax.nn.log_sigmoid((gate_lr @ w_alpha + b_alpha).astype(jnp.float32)) / GATE_LOGIT_NORMALIZER

    def heads(t, d):
        return t.astype(jnp.float32).reshape(B, N, CHUNK, GLA_HEADS, d).transpose(0, 3, 1, 2, 4)

    qh = heads(q, GLA_DK) * (GLA_DK ** -0.5)
    kh = heads(k, GLA_DK)
    vh = heads(v, GLA_DV)
    bcum = jnp.cumsum(heads(log_alpha, GLA_DK), axis=3)
    b_last = bcum[:, :, :, -1:]
    q_dec = qh * jnp.exp(bcum)
    k_inv = kh * jnp.exp(-bcum)
    k_tail = kh * jnp.exp(b_last - bcum)

    mask = jnp.tril(jnp.ones((CHUNK, CHUNK), dtype=bool))
    scores = jnp.where(mask, jnp.einsum('bhnid,bhnjd->bhnij', q_dec, k_inv), 0.0)
    o_intra = jnp.einsum('bhnij,bhnjv->bhniv', scores, vh)

    kv_chunk = jnp.einsum('bhncd,bhncv->bhndv', k_tail, vh)
    decay_chunk = jnp.exp(b_last[:, :, :, 0])

    def step(state, inp):
        dec, kv = inp
        return state * dec[..., None] + kv, state

    init = jnp.zeros((B, GLA_HEADS, GLA_DK, GLA_DV), jnp.float32)
    _, states = lax.scan(step, init, (decay_chunk.transpose(2, 0, 1, 3), kv_chunk.transpose(2, 0, 1, 3, 4)))
    states = states.transpose(1, 2, 0, 3, 4)
    o = o_intra + jnp.einsum('bhncd,bhndv->bhncv', q_dec, states)

    o = o * lax.rsqrt(jnp.mean(o * o, axis=-1, keepdims=True) + EPS) * gla_norm.astype(jnp.float32)
    o = o.transpose(0, 2, 3, 1, 4).reshape(B, S, D_GLA)
    return (o * jax.nn.silu(g_out.astype(jnp.float32))).astype(q.dtype)


def setup_inputs(seed: int = 0) -> dict:
    key = jax.random.key(seed)
    ks = jax.random.split(key, 16)
    f32 = jnp.float32

    def nrm(k, shape, fan_in):
        return jax.random.normal(k, shape, f32) * (fan_in ** -0.5)

    def gain(k, shape):
        return 1.0 + 0.02 * jax.random.normal(k, shape, f32)

    L = DEPTH
    return {
        "x": jax.random.normal(ks[0], (BATCH, SEQ, D_MODEL), f32),
        "ffn1_norm": gain(ks[1], (L, D_MODEL)),
        "ffn1_w_in": nrm(ks[2], (L, D_MODEL, 2 * D_FF), D_MODEL),
        "ffn1_w_out": nrm(ks[3], (L, D_FF, D_MODEL), D_FF),
        "mix_norm": gain(ks[4], (L, D_MODEL)),
        "w_in_mix": nrm(ks[5], (L, D_MODEL, D_IN), D_MODEL),
        "w_pool": nrm(ks[6], (L, N_POOL_GROUPS, POOL_GROUP_DIM, POOL_GROUP_DIM), POOL_GROUP_DIM),
        "pool_scale": 1.0 + 0.1 * jax.random.normal(ks[7], (L, D_POOL), f32),
        "w_alpha": nrm(ks[8], (L, GLA_GATE_RANK, GLA_DK_TOTAL), GLA_GATE_RANK),
        "b_alpha": 0.01 * jax.random.normal(ks[9], (L, GLA_DK_TOTAL), f32),
        "gla_norm": gain(ks[10], (L, GLA_DV)),
        "w_out_mix": nrm(ks[11], (L, D_MIX, D_MODEL), D_MIX),
        "ffn2_norm": gain(ks[12], (L, D_MODEL)),
        "ffn2_w_in": nrm(ks[13], (L, D_MODEL, 2 * D_FF), D_MODEL),
        "ffn2_w_out": nrm(ks[14], (L, D_FF, D_MODEL), D_FF),
        "final_norm": gain(ks[15], (D_MODEL,)),
    }


def reference(x, ffn1_norm, ffn1_w_in, ffn1_w_out, mix_norm, w_in_mix, w_pool, pool_scale,
              w_alpha, b_alpha, gla_norm, w_out_mix, ffn2_norm, ffn2_w_in, ffn2_w_out, final_norm):
    h = x
    o_q = D_POOL
    o_k = o_q + GLA_DK_TOTAL
    o_v = o_k + GLA_DK_TOTAL
    o_g = o_v + D_GLA
    o_r = o_g + D_GLA
    for l in range(DEPTH):
        h = h + 0.5 * swiglu(rmsnorm(h, ffn1_norm[l]), ffn1_w_in[l], ffn1_w_out[l])
        u = rmsnorm(h, mix_norm[l]) @ w_in_mix[l]
        y_pool = pool_mixer(u[..., :o_q], w_pool[l], pool_scale[l])
        y_gla = gla_mixer(u[..., o_q:o_k], u[..., o_k:o_v], u[..., o_v:o_g], u[..., o_g:o_r],
                          u[..., o_r:], w_alpha[l], b_alpha[l], gla_norm[l])
        h = h + jnp.concatenate([y_pool.astype(h.dtype), y_gla.astype(h.dtype)], axis=-1) @ w_out_mix[l]
        h = h + 0.5 * swiglu(rmsnorm(h, ffn2_norm[l]), ffn2_w_in[l], ffn2_w_out[l])
    return rmsnorm(h, final_norm)
```

```python
import numpy as np
from contextlib import ExitStack
import concourse.bass as bass
import concourse.mybir as mybir
from concourse.bass_utils import run_bass_kernel_spmd

F32 = mybir.dt.float32
BF16 = mybir.dt.bfloat16
AF = mybir.ActivationFunctionType
ALU = mybir.AluOpType

NCORES = 8
D = 2048
T = 1024
DFF = 5632
NFC = 44
GRP = 4
NGRP = NFC // GRP
EPS = 1e-6


class Res:
    __slots__ = ("w", "r")

    def __init__(self):
        self.w = None
        self.r = {}


class Sched:
    def __init__(self, nc, es):
        self.nc = nc
        self.es = es
        self.E = {"pe": nc.tensor, "act": nc.scalar, "dve": nc.vector, "pool": nc.gpsimd, "sp": nc.sync}
        self.sem = {}
        self.cnt = {}
        for e in self.E:
            self.sem[e] = es.enter_context(nc.semaphore("s_" + e))
            self.cnt[e] = 0
        self.waited = {e: {} for e in self.E}

    def dsem(self, name):
        self.sem[name] = self.es.enter_context(self.nc.semaphore(name))
        self.cnt[name] = 0
        return name

    def _wait(self, e, deps):
        for k, v in deps.items():
            if k == e and e == "pe":
                continue
            if self.waited[e].get(k, 0) >= v:
                continue
            self.E[e].wait_ge(self.sem[k], v)
            self.waited[e][k] = v

    def op(self, e, fn, reads=(), writes=(), dma=None, cc=False):
        deps = {}

        def add(sig):
            if sig is None:
                return
            k, v = sig
            if deps.get(k, 0) < v:
                deps[k] = v

        for r in reads:
            add(r.w)
        for w in writes:
            add(w.w)
            for k, v in w.r.items():
                add((k, v))
        self._wait(e, deps)
        inst = fn()
        if cc:
            inst.then_inc(self.sem[dma])
            self.cnt[dma] += 1
            sig = (dma, self.cnt[dma])
        elif dma is None:
            self.cnt[e] += 1
            inst.then_inc(self.sem[e], 1)
            sig = (e, self.cnt[e])
        else:
            self.cnt[dma] += 16
            inst.then_inc(self.sem[dma], 16)
            sig = (dma, self.cnt[dma])
        for r in reads:
            if r.r.get(sig[0], 0) < sig[1]:
                r.r[sig[0]] = sig[1]
        for w in writes:
            w.w = sig
            w.r = {}
        return sig

    def barrier(self):
        allv = {k: v for k, v in self.cnt.items() if v > 0}
        for e in self.E:
            d = {k: v for k, v in allv.items() if k != e}
            for k, v in d.items():
                if self.waited[e].get(k, 0) >= v:
                    continue
                self.E[e].wait_ge(self.sem[k], v)
                self.waited[e][k] = v


class Rot:
    def __init__(self, tiles, sems=None):
        self.t = tiles
        self.R = [Res() for _ in tiles]
        self.sems = sems
        self.i = 0
        self.k = 0

    def nxt(self):
        k = self.i % len(self.t)
        self.k = k
        self.i += 1
        return self.t[k], self.R[k]

    def sem(self):
        return self.sems[self.k]


def build_nc():
    nc = bass.Bass("TRN2", target_bir_lowering=False)

    def din(name, shape):
        return nc.dram_tensor(name, shape, F32, kind="ExternalInput").ap()

    xT = din("xT", [D, T])
    w1i = din("w1i", [NFC, 128, 4096])
    w1o = din("w1o", [DFF, D])
    w2i = din("w2i", [NFC, 128, 4096])
    w2o = din("w2o", [DFF, D])
    wmx = din("wmx", [32, 128, 2048])
    wmr = din("wmr", [128, 256])
    wom = din("wom", [D, D])
    wpl = din("wpl", [1024, 256])
    wal = din("wal", [16, 512])
    vecs = din("vecs", [128, 80])
    cvec = din("cvec", [128, 72])
    cst = din("cst", [128, 256])
    outT = nc.dram_tensor("outT", [D, T], F32, kind="ExternalOutput").ap()
    cin = nc.dram_tensor("cin", [128, 1024], F32)
    cout = nc.dram_tensor("cout", [NCORES * 128, 1024], F32)
    cinh = nc.dram_tensor("cinh", [128, 256], BF16)
    couth = nc.dram_tensor("couth", [NCORES * 128, 256], BF16)

    with ExitStack() as es:
        S = Sched(nc, es)

        def sb(name, shape, dt, stack=es):
            return stack.enter_context(nc.sbuf_tensor(name, shape, dt))

        hT = sb("hT", [128, 16, T], F32)
        nT = sb("nT", [128, 16, T], BF16)
        hTR = [Res() for _ in range(16)]
        nTR = [Res(), Res()]
        ones_bf = sb("ones_bf", [128, 128], BF16)
        ident_bf = sb("ident_bf", [128, 128], BF16)
        maskT_bf = sb("maskT_bf", [128, 128], BF16)
        vec = sb("vec", [128, 80], F32)
        cv = sb("cv", [128, 72], F32)
        eps_t = sb("eps_t", [128, 1], F32)
        one_t = sb("one_t", [128, 1], F32)
        nb_t = sb("nb_t", [128, 4], F32)
        constR = Res()
        sqrot = Rot([sb(f"sq{i}", [128, 512], BF16) for i in range(3)])
        srot = Rot([sb(f"st{i}", [128, 512], F32) for i in range(1)])
        rrot = Rot([sb(f"rt{i}", [128, 512], F32) for i in range(2)])

        PS = [es.enter_context(nc.psum_tensor(f"ps{i}", [128, 512], F32)) for i in range(7)]
        PSR = [Res() for _ in range(7)]
        psb = es.enter_context(nc.psum_tensor("psb", [128, 1024], BF16))
        psbR = Res()

        d_x = [S.dsem(f"d_x{i}") for i in range(16)]
        d_c = S.dsem("d_c")
        d_cp = S.dsem("d_cp")
        d_w = [S.dsem(f"d_w{i}") for i in range(4)]
        d_wo = [S.dsem(f"d_wo{i}") for i in range(2)]
        d_m = [S.dsem(f"d_m{i}") for i in range(3)]
        d_wr = S.dsem("d_wr")
        d_pl = [S.dsem(f"d_pl{i}") for i in range(2)]
        d_o = [S.dsem(f"d_o{i}") for i in range(2)]
        d_st = [S.dsem(f"d_st{i}") for i in range(2)]
        d_sth = S.dsem("d_sth")
        d_cc = S.dsem("d_cc")
        d_g = S.dsem("d_g")

        S.op("sp", lambda: nc.sync.dma_start(out=vec[:], in_=vecs[:, :]), writes=[constR], dma=d_c)
        S.op("sp", lambda: nc.sync.dma_start(out=cv[:], in_=cvec[:, :]), writes=[constR], dma=d_c)
        S.op("pool", lambda: nc.gpsimd.dma_start(out=ident_bf[:], in_=cst[:, 0:128]), writes=[constR], dma=d_cp)
        S.op("pool", lambda: nc.gpsimd.dma_start(out=maskT_bf[:], in_=cst[:, 128:256]), writes=[constR], dma=d_cp)
        S.op("dve", lambda: nc.vector.memset(ones_bf[:], 1.0), writes=[constR])
        S.op("dve", lambda: nc.vector.memset(eps_t[:], EPS), writes=[constR])
        S.op("dve", lambda: nc.vector.memset(one_t[:], 1.0), writes=[constR])
        S.op("dve", lambda: nc.vector.tensor_scalar(out=nb_t[:], in0=vec[:, 72:76], scalar1=-1.0, scalar2=None,
                                                    op0=ALU.mult), reads=[constR], writes=[constR])

        for j in range(16):
            S.op("sp", lambda: nc.sync.dma_start(out=hT[:, j, :], in_=xT[j * 128:(j + 1) * 128, :]),
                 writes=[hTR[j]], dma=d_x[j])

        def emit_stat(j, t, banks):
            sl = slice(t * 512, (t + 1) * 512)
            pst, pstR = banks[t]
            sq, sqR = sqrot.nxt()
            S.op("act", lambda: nc.scalar.activation(out=sq[:], in_=hT[:, j, sl], func=AF.Square),
                 reads=[hTR[j]], writes=[sqR])
            S.op("pe", lambda: nc.tensor.matmul(pst[:], ones_bf[:], sq[:], start=(j == 0), stop=(j == 15)),
                 reads=[sqR, constR], writes=[pstR])

        class Lag:
            def __init__(self, banks, depth=3):
                self.banks, self.depth, self.q = banks, depth, []

            def push(self, j, t):
                self.q.append((j, t))
                if len(self.q) > self.depth:
                    emit_stat(*self.q.pop(0), self.banks)

            def flush(self):
                while self.q:
                    emit_stat(*self.q.pop(0), self.banks)

        def rmsnorm(out_cb, pre=None):
            for t in range(2):
                sl = slice(t * 512, (t + 1) * 512)
                if pre is None:
                    banks = [(PS[6], PSR[6]), (PS[6], PSR[6])]
                    for j in range(16):
                        emit_stat(j, t, banks)
                    pst, pstR = PS[6], PSR[6]
                else:
                    pst, pstR = pre[t]
                st, stR = srot.nxt()
                S.op("act", lambda: nc.scalar.activation(out=st[:], in_=pst[:], func=AF.Sqrt, bias=eps_t[:, 0:1],
                                                         scale=1.0 / D), reads=[pstR, constR], writes=[stR])
                rt, rtR = rrot.nxt()
                S.op("dve", lambda: nc.vector.reciprocal(out=rt[:], in_=st[:]), reads=[stR], writes=[rtR])
                for j in range(16):
                    out_cb(j, t, sl, rt, rtR)

        def norm_to_nT(gcol, pre=None):
            def cb(j, t, sl, rt, rtR):
                S.op("dve", lambda: nc.vector.scalar_tensor_tensor(
                    out=nT[:, j, sl], in0=hT[:, j, sl], scalar=vec[:, gcol + j:gcol + j + 1], in1=rt[:],
                    op0=ALU.mult, op1=ALU.mult), reads=[hTR[j], rtR, constR], writes=[nTR[t]])
            rmsnorm(cb, pre)

        tail_banks = [(PS[0], PSR[0]), (PS[1], PSR[1])]
        mix_banks = [(PS[6], PSR[6]), (PS[5], PSR[5])]

        def ffn(tag, w_in_d, w_out_d, gcol, pre=None):
            with ExitStack() as fs:
                winb = [sb(f"{tag}win{i}", [128, 4096], BF16, fs) for i in range(3)]
                winR = [Res() for _ in range(3)]
                woutb = [sb(f"{tag}wout{i}", [128, GRP, D], BF16, fs) for i in range(2)]
                woutR = [Res() for _ in range(2)]
                actT = [sb(f"{tag}act{i}", [128, GRP, T], BF16, fs) for i in range(2)]
                actR = [Res() for _ in range(2)]
                sgrot = Rot([sb(f"{tag}sg{i}", [128, 512], F32, fs) for i in range(2)])

                def load_win(c):
                    k = c % 3
                    S.op("pool", lambda: nc.gpsimd.dma_start(out=winb[k][:], in_=w_in_d[c]),
                         writes=[winR[k]], dma=d_w[k])

                def load_wout(g, after=()):
                    k = g % 2
                    S.op("pool", lambda: nc.gpsimd.dma_start(
                        out=woutb[k][:],
                        in_=w_out_d[g * 512:(g + 1) * 512, :].rearrange("(c p) d -> p c d", p=128)),
                        reads=list(after), writes=[woutR[k]], dma=d_wo[k])

                load_win(0)
                load_win(1)
                load_wout(0, after=[hTR[15]])
                norm_to_nT(gcol, pre)

                def phaseA(g):
                    a = g % 2
                    for cl in range(GRP):
                        c = g * GRP + cl
                        k = c % 3
                        wb = winb[k]
                        if c + 2 < NFC:
                            load_win(c + 2)
                        for t in range(2):
                            sl = slice(t * 512, (t + 1) * 512)
                            pg, pgR = PS[t], PSR[t]
                            pu, puR = PS[2 + t], PSR[2 + t]

                            def grp(ps, gu):
                                last = None
                                for kc in range(16):
                                    o = (gu * 16 + kc) * 128
                                    last = nc.tensor.matmul(ps[:], wb[:, o:o + 128], nT[:, kc, sl],
                                                            start=(kc == 0), stop=(kc == 15))
                                return last

                            S.op("pe", lambda: grp(pg, 0), reads=[winR[k], nTR[t]], writes=[pgR])
                            S.op("pe", lambda: grp(pu, 1), reads=[winR[k], nTR[t]], writes=[puR])
                            sg, sgR = sgrot.nxt()
                            S.op("act", lambda: nc.scalar.activation(out=sg[:], in_=pg[:], func=AF.Silu),
                                 reads=[pgR], writes=[sgR])
                            S.op("dve", lambda: nc.vector.tensor_tensor(out=actT[a][:, cl, sl], in0=pu[:], in1=sg[:],
                                                                        op=ALU.mult),
                                 reads=[puR, sgR], writes=[actR[a]])

                def phaseB(g, lag=None):
                    a = g % 2
                    k = g % 2
                    for j in range(16):
                        for t in range(2):
                            sl = slice(t * 512, (t + 1) * 512)
                            po, poR = PS[4 + t], PSR[4 + t]

                            def grp():
                                last = None
                                for cl in range(GRP):
                                    last = nc.tensor.matmul(po[:], woutb[k][:, cl, j * 128:(j + 1) * 128],
                                                            actT[a][:, cl, sl], start=(cl == 0), stop=(cl == GRP - 1))
                                return last

                            S.op("pe", grp, reads=[woutR[k], actR[a]], writes=[poR])
                            S.op("dve", lambda: nc.vector.scalar_tensor_tensor(
                                out=hT[:, j, sl], in0=po[:], scalar=0.5, in1=hT[:, j, sl],
                                op0=ALU.mult, op1=ALU.add), reads=[poR, hTR[j]], writes=[hTR[j]])
                            if lag is not None:
                                lag.push(j, t)
                    if lag is not None:
                        lag.flush()

                for g in range(NGRP):
                    phaseA(g)
                    if g >= 1:
                        phaseB(g - 1)
                    if g + 1 < NGRP:
                        load_wout(g + 1)
                phaseB(NGRP - 1, Lag(tail_banks))
                S.barrier()

        def mixer():
            with ExitStack() as ms:
                wrot = Rot([sb(f"mw{i}", [128, 2048], BF16, ms) for i in range(2)], d_m)
                wr_t = sb("wr_t", [128, 256], BF16, ms)
                wal_t = sb("wal_t", [16, 512], F32, ms)
                wplrot = Rot([sb(f"wpl{i}", [128, 2, 256], BF16, ms) for i in range(1)], d_pl)
                womrot = Rot([sb(f"wom{i}", [128, 2, D], BF16, ms) for i in range(1)], d_o)
                glrT = sb("glrT", [16, T], F32, ms)
                FA = sb("FA", [128, 2080], F32, ms)
                FB = sb("FB", [128, 2080], F32, ms)
                carry = sb("carry", [128, 1024], F32, ms)
                halo_in = sb("halo_in", [128, 256], BF16, ms)
                halo_o = sb("halo_o", [128, 256], BF16, ms)
                stghrot = Rot([sb("stgh", [128, 256], BF16, ms)], [d_sth])
                q_decs = [sb(f"q_dec{h}", [128, T], BF16, ms) for h in range(4)]
                k_toks = [sb(f"k_tok{h}", [128, 8, 128], BF16, ms) for h in range(4)]
                scTs = [sb(f"scT{h}", [128, 8, 128], BF16, ms) for h in range(4)]
                decs = [sb(f"dec{h}", [128, 16], F32, ms) for h in range(4)]
                v_toks = [sb(f"v_tok{i}", [128, 8, 256], BF16, ms) for i in range(3)]
                RV = [Res() for _ in range(3)]
                vslot = {0: 0, 1: 1, 2: 2, 3: 0}
                L_bf = sb("L_bf", [128, 8, 256], BF16, ms)
                k_invT = L_bf[:].rearrange("p a b -> p (a b)")[:, 0:T]
                Lp = [sb(f"Lp{i}", [128, 256], F32, ms) for i in range(2)]
                LpR = [Res(), Res()]
                lcur = [0]
                kvdrot = Rot([sb(f"kvd{i}", [128, 256], F32, ms) for i in range(2)])
                sqb = sb("sqb", [128, 2, T], BF16, ms)
                ypc = sb("ypc", [128, 2, T], BF16, ms)
                blast = sb("blast", [128, 16], F32, ms)
                smask = sb("smask", [128, T], BF16, ms)
                R = {n: Res() for n in ["wr", "wal", "glr", "FA", "FB", "carry", "halo_in", "k_invT",
                                        "L_bf", "sqb", "ypc", "halo_o",
                                        "blast", "smask", "cin", "cout", "cinh", "couth"]}
                for h in range(4):
                    for n in ("q_dec", "k_tok", "scT", "dec"):
                        R[f"{n}{h}"] = Res()
                psrot_i = [0]
                npsmod = [6]

                def nps():
                    k = psrot_i[0] % npsmod[0]
                    psrot_i[0] += 1
                    return PS[k], PSR[k]

                def load_w(c):
                    wt, wR = wrot.nxt()
                    S.op("pool", lambda: nc.gpsimd.dma_start(out=wt[:], in_=wmx[c]), writes=[wR], dma=wrot.sem())
                    return wt, wR

                S.op("pool", lambda: nc.gpsimd.dma_start(out=wr_t[:], in_=wmr[:, :]), writes=[R["wr"]], dma=d_wr)
                S.op("sp", lambda: nc.sync.dma_start(out=wal_t[:], in_=wal[:, :]), writes=[R["wal"]], dma=d_c)
                S.op("dve", lambda: nc.vector.memset(smask[:], 1.0), writes=[R["smask"]])
                S.op("dve", lambda: nc.vector.memset(smask[:].rearrange("p (n c) -> p n c", c=64)[:, :, 0:1], 0.0),
                     writes=[R["smask"]])

                norm_to_nT(16, tail_banks)

                def proj_gen(wt, wR, evac):
                    for t in range(2):
                        sl = slice(t * 512, (t + 1) * 512)
                        ps, psR = nps()

                        def grp():
                            last = None
                            for kc in range(16):
                                last = nc.tensor.matmul(ps[:], wt[:, kc * 128:(kc + 1) * 128], nT[:, kc, sl],
                                                        start=(kc == 0), stop=(kc == 15))
                            return last

                        S.op("pe", grp, reads=[wR, nTR[t]], writes=[psR])
                        evac(t, sl, ps, psR)
                        yield

                def proj_fm(wt, wR, evac):
                    for _ in proj_gen(wt, wR, evac):
                        pass

                def inproj_gen(c, evac):
                    wt, wR = load_w(c)
                    yield from proj_gen(wt, wR, evac)

                def inproj_fm(c, evac):
                    for _ in inproj_gen(c, evac):
                        pass

                def interleave(ga, gb):
                    ga = iter(ga) if ga is not None else iter(())
                    gb = iter(gb) if gb is not None else iter(())
                    da = db = False
                    while not (da and db):
                        if not da:
                            try:
                                next(ga)
                            except StopIteration:
                                da = True
                        if not db:
                            try:
                                next(gb)
                            except StopIteration:
                                db = True

                S.op("pool", lambda: nc.gpsimd.tensor_copy(out=halo_o[:].rearrange("p (c t) -> p c t", t=16),
                                                           in_=nT[:, :, T - 16:T]),
                     reads=[nTR[1]], writes=[R["halo_o"]])
                S.op("sp", lambda: nc.sync.dma_start(out=cinh[:, :], in_=halo_o[:]), reads=[R["halo_o"]],
                     writes=[R["cinh"]], dma=d_g)
                S.op("pool", lambda: nc.gpsimd.collective_compute(
                    "AllGather", mybir.AluOpType.bypass, replica_groups=[list(range(NCORES))],
                    ins=[cinh.ap().opt()], outs=[couth.ap().opt()]), reads=[R["cinh"]], writes=[R["couth"]],
                    dma=d_cc, cc=True)

                for t in range(2):
                    sl = slice(t * 512, (t + 1) * 512)
                    ps, psR = nps()

                    def grp():
                        last = None
                        for kc in range(16):
                            last = nc.tensor.matmul(ps[0:16, :], wr_t[:, kc * 16:(kc + 1) * 16], nT[:, kc, sl],
                                                    start=(kc == 0), stop=(kc == 15))
                        return last

                    S.op("pe", grp, reads=[R["wr"], nTR[t]], writes=[psR])
                    S.op("act", lambda: nc.scalar.copy(out=glrT[:, sl], in_=ps[0:16, :]), reads=[psR],
                         writes=[R["glr"]])

                lA = FA[:, 0:T]
                csl = FB[:, 0:T]
                E1 = FB[:, 1040:1040 + T]
                E2 = FA[:, 1040:1040 + T]

                def vproj(h):
                    v_tok, vR = v_toks[vslot[h]], RV[vslot[h]]
                    for dvc in range(2):
                        wt, wR = load_w(16 + h * 2 + dvc)
                        for hb in range(2):
                            ps, psR = nps()

                            def grp():
                                last = None
                                for tl in range(4):
                                    tt = hb * 4 + tl
                                    for kc in range(16):
                                        last = nc.tensor.matmul(ps[:, tl * 128:(tl + 1) * 128],
                                                                nT[:, kc, tt * 128:(tt + 1) * 128],
                                                                wt[:, kc * 128:(kc + 1) * 128],
                                                                start=(kc == 0), stop=(kc == 15))
                                return last

                            S.op("pe", grp, reads=[wR, nTR[0], nTR[1]], writes=[psR])
                            S.op("act", lambda: nc.scalar.copy(
                                out=v_tok[:, hb * 4:(hb + 1) * 4, dvc * 128:(dvc + 1) * 128],
                                in_=ps[:].rearrange("p (a b) -> p a b", b=128)), reads=[psR], writes=[vR])

                def recur_gen(h, n0, n1, keep, every=2):
                    k_tok, dec = k_toks[h], decs[h]
                    v_tok, vR = v_toks[vslot[h]], RV[vslot[h]]
                    for n in range(n0, n1):
                        tt, off = n // 2, (n % 2) * 64
                        ps, psR = nps()
                        S.op("pe", lambda: nc.tensor.matmul(ps[:, 0:256], k_tok[off:off + 64, tt, :],
                                                            v_tok[off:off + 64, tt, :], start=True, stop=True),
                             reads=[R[f"k_tok{h}"], vR], writes=[psR])
                        kvd, kvdR = kvdrot.nxt()
                        S.op("act", lambda: nc.scalar.activation(out=kvd[:], in_=ps[:, 0:256], func=AF.Copy,
                                                                 scale=dec[:, n:n + 1]),
                             reads=[psR, R[f"dec{h}"]], writes=[kvdR])
                        cur, curR = Lp[lcur[0]], LpR[lcur[0]]
                        nxt_, nxtR = Lp[1 - lcur[0]], LpR[1 - lcur[0]]
                        if keep:
                            S.op("act", lambda: nc.scalar.copy(out=L_bf[:, n - n0, :], in_=cur[:]), reads=[curR],
                                 writes=[R["L_bf"]])
                        S.op("dve", lambda: nc.vector.scalar_tensor_tensor(
                            out=nxt_[:], in0=cur[:], scalar=dec[:, n:n + 1], in1=kvd[:],
                            op0=ALU.mult, op1=ALU.add), reads=[curR, kvdR, R[f"dec{h}"]], writes=[nxtR])
                        lcur[0] = 1 - lcur[0]
                        if (n - n0) % every == every - 1:
                            yield

                def head_decay(h):
                    q_dec, k_tok, scT, dec = q_decs[h], k_toks[h], scTs[h], decs[h]
                    for t in range(2):
                        sl = slice(t * 512, (t + 1) * 512)
                        ps, psR = nps()
                        S.op("pe", lambda: nc.tensor.matmul(ps[:], wal_t[0:16, h * 128:(h + 1) * 128], glrT[0:16, sl],
                                                            start=True, stop=True),
                             reads=[R["wal"], R["glr"]], writes=[psR])
                        S.op("act", lambda: nc.scalar.activation(out=lA[:, sl], in_=ps[:], func=AF.Exp,
                                                                 bias=nb_t[:, h:h + 1], scale=-1.0),
                             reads=[psR, constR], writes=[R["FA"]])
                    S.op("act", lambda: nc.scalar.activation(out=lA, in_=lA, func=AF.Ln, bias=one_t[:, 0:1], scale=1.0),
                         reads=[R["FA"], constR], writes=[R["FA"]])
                    S.op("dve", lambda: nc.vector.tensor_tensor_scan(out=csl, data0=smask[:], data1=lA, initial=0.0,
                                                                     op0=ALU.mult, op1=ALU.add),
                         reads=[R["smask"], R["FA"]], writes=[R["FB"]])
                    csl_v = csl.rearrange("p (n c) -> p n c", c=64)
                    S.op("dve", lambda: nc.vector.tensor_copy(out=blast[:].rearrange("p (n o) -> p n o", o=1),
                                                              in_=csl_v[:, :, 63:64]),
                         reads=[R["FB"]], writes=[R["blast"]])
                    S.op("act", lambda: nc.scalar.activation(out=E1, in_=csl, func=AF.Exp, scale=-1.0 / 16.0),
                         reads=[R["FB"]], writes=[R["FB"]])
                    S.op("act", lambda: nc.scalar.activation(out=E2, in_=csl, func=AF.Exp, scale=1.0 / 16.0),
                         reads=[R["FB"]], writes=[R["FA"]])
                    S.op("act", lambda: nc.scalar.activation(out=dec[:], in_=blast[:], func=AF.Exp, scale=-1.0 / 16.0),
                         reads=[R["blast"]], writes=[R[f"dec{h}"]])

                def head_qk_gen(h):
                    q_dec, k_tok, scT, dec = q_decs[h], k_toks[h], scTs[h], decs[h]
                    def ev_q(t, sl, ps, psR):
                        S.op("dve", lambda: nc.vector.scalar_tensor_tensor(
                            out=q_dec[:, sl], in0=ps[:], scalar=float(128 ** -0.5), in1=E1[:, sl],
                            op0=ALU.mult, op1=ALU.mult), reads=[psR, R["FB"]], writes=[R[f"q_dec{h}"]])

                    def ev_k(t, sl, ps, psR):
                        S.op("dve", lambda: nc.vector.tensor_tensor(out=k_invT[:, sl], in0=ps[:], in1=E2[:, sl],
                                                                    op=ALU.mult),
                             reads=[psR, R["FA"]], writes=[R["L_bf"]])

                    yield from inproj_gen(8 + h, ev_q)
                    yield from inproj_gen(12 + h, ev_k)

                def head_small(h):
                    q_dec, k_tok, scT, dec = q_decs[h], k_toks[h], scTs[h], decs[h]
                    def tr():
                        last = None
                        for tt in range(8):
                            last = nc.tensor.transpose(psb[:, tt * 128:(tt + 1) * 128],
                                                       k_invT[:, tt * 128:(tt + 1) * 128], ident_bf[:])
                        return last
                    S.op("pe", tr, reads=[R["L_bf"], constR], writes=[psbR])
                    S.op("act", lambda: nc.scalar.copy(out=k_tok[:].rearrange("p a b -> p (a b)"), in_=psb[:]),
                         reads=[psbR], writes=[R[f"k_tok{h}"]])
                    for hb in range(2):
                        ps, psR = nps()

                        def grp():
                            last = None
                            for tl in range(4):
                                tt = hb * 4 + tl
                                ts_ = slice(tt * 128, (tt + 1) * 128)
                                last = nc.tensor.matmul(ps[:, tl * 128:(tl + 1) * 128], k_invT[:, ts_],
                                                        q_dec[:, ts_], start=True, stop=True)
                            return last

                        S.op("pe", grp, reads=[R["L_bf"], R[f"q_dec{h}"]], writes=[psR])
                        for tl in range(4):
                            tt = hb * 4 + tl
                            S.op("dve", lambda: nc.vector.tensor_tensor(out=scT[:, tt, :],
                                                                        in0=ps[:, tl * 128:(tl + 1) * 128],
                                                                        in1=maskT_bf[:], op=ALU.mult),
                                 reads=[psR, constR], writes=[R[f"scT{h}"]])

                def head_state_gen(h):
                    S.op("dve", lambda: nc.vector.memset(Lp[lcur[0]][:], 0.0), writes=[LpR[lcur[0]]])
                    yield from recur_gen(h, 0, 16, False, every=4)
                    S.op("sp", lambda: nc.sync.dma_start(out=cin[:, h * 256:(h + 1) * 256], in_=Lp[lcur[0]][:]),
                         reads=[LpR[lcur[0]]], writes=[R["cin"]], dma=d_g)

                def outproj_gen(piece, lag=None):
                    wo, woR = womrot.nxt()
                    S.op("pool", lambda: nc.gpsimd.dma_start(
                        out=wo[:], in_=wom[piece * 256:(piece + 1) * 256, :].rearrange("(c p) d -> p c d", p=128)),
                        writes=[woR], dma=womrot.sem())
                    for j in range(16):
                        for t in range(2):
                            sl = slice(t * 512, (t + 1) * 512)
                            ps, psR = nps()

                            def grp():
                                nc.tensor.matmul(ps[:], wo[:, 0, j * 128:(j + 1) * 128], ypc[:, 0, sl],
                                                 start=True, stop=False)
                                return nc.tensor.matmul(ps[:], wo[:, 1, j * 128:(j + 1) * 128], ypc[:, 1, sl],
                                                        start=False, stop=True)

                            S.op("pe", grp, reads=[woR, R["ypc"]], writes=[psR])
                            S.op("dve", lambda: nc.vector.tensor_tensor(out=hT[:, j, sl], in0=ps[:], in1=hT[:, j, sl],
                                                                        op=ALU.add),
                                 reads=[psR, hTR[j]], writes=[hTR[j]])
                            if lag is not None:
                                lag.push(j, t)
                        if j == 15 and lag is not None:
                            lag.flush()
                        if j % 4 == 3:
                            yield

                def outproj(piece, hook=None):
                    for _ in outproj_gen(piece):
                        if hook is not None:
                            hook()

                def drain(gen, n):
                    if gen is None:
                        return
                    for _ in range(n):
                        try:
                            next(gen)
                        except StopIteration:
                            return

                def head_back(h, pending):
                    q_dec, scT = q_decs[h], scTs[h]
                    v_tok, vR = v_toks[vslot[h]], RV[vslot[h]]
                    oT = FA[:, 0:2 * T].rearrange("p (a b) -> p a b", b=T)
                    sgT = FB[:, 0:2 * T].rearrange("p (a b) -> p a b", b=T)
                    if h == 0:
                        vproj(h)

                    def gproj_gen():
                        for dvc in range(2):
                            def ev_g(t, sl, ps, psR):
                                S.op("act", lambda: nc.scalar.activation(out=sgT[:, dvc, sl], in_=ps[:], func=AF.Silu),
                                     reads=[psR], writes=[R["FB"]])
                            yield from inproj_gen(24 + h * 2 + dvc, ev_g)
                    S.op("dve", lambda: nc.vector.tensor_copy(out=Lp[lcur[0]][:], in_=carry[:, h * 256:(h + 1) * 256]),
                         reads=[R["carry"]], writes=[LpR[lcur[0]]])
                    for t in range(2):
                        sl = slice(t * 512, (t + 1) * 512)
                        interleave(recur_gen(h, t * 8, t * 8 + 8, True, every=2),
                                   pending if t == 0 else gproj_gen())
                        for dvc in range(2):
                            dsl = slice(dvc * 128, (dvc + 1) * 128)
                            ps, psR = nps()

                            def grp():
                                for tl in range(4):
                                    tt = t * 4 + tl
                                    nc.tensor.matmul(ps[:, tl * 128:(tl + 1) * 128], v_tok[:, tt, dsl], scT[:, tt, :],
                                                     start=(tl == 0), stop=False)
                                last = None
                                for nl in range(8):
                                    n = t * 8 + nl
                                    last = nc.tensor.matmul(ps[:, nl * 64:(nl + 1) * 64], L_bf[:, nl, dsl],
                                                            q_dec[:, n * 64:(n + 1) * 64], start=False, stop=(nl == 7))
                                return last

                            S.op("pe", grp, reads=[vR, R[f"scT{h}"], R["L_bf"], R[f"q_dec{h}"]], writes=[psR])
                            S.op("act", lambda: nc.scalar.copy(out=oT[:, dvc, sl], in_=ps[:]), reads=[psR],
                                 writes=[R["FA"]])
                            S.op("act", lambda: nc.scalar.activation(out=sqb[:, dvc, sl], in_=oT[:, dvc, sl],
                                                                     func=AF.Square), reads=[R["FA"]], writes=[R["sqb"]])
                    drain(pending, 4)
                    rts = []
                    for t in range(2):
                        sl = slice(t * 512, (t + 1) * 512)
                        pst, pstR = PS[6], PSR[6]

                        def grp():
                            nc.tensor.matmul(pst[:], ones_bf[:], sqb[:, 0, sl], start=True, stop=False)
                            return nc.tensor.matmul(pst[:], ones_bf[:], sqb[:, 1, sl], start=False, stop=True)

                        S.op("pe", grp, reads=[R["sqb"], constR], writes=[pstR])
                        st, stR = srot.nxt()
                        S.op("act", lambda: nc.scalar.activation(out=st[:], in_=pst[:], func=AF.Sqrt,
                                                                 bias=eps_t[:, 0:1], scale=1.0 / 256.0),
                             reads=[pstR, constR], writes=[stR])
                        rt, rtR = rrot.nxt()
                        S.op("dve", lambda: nc.vector.reciprocal(out=rt[:], in_=st[:]), reads=[stR], writes=[rtR])
                        rts.append((rt, rtR))
                    for dvc in range(2):
                        for t in range(2):
                            sl = slice(t * 512, (t + 1) * 512)
                            rt, rtR = rts[t]
                            S.op("dve", lambda: nc.vector.scalar_tensor_tensor(
                                out=oT[:, dvc, sl], in0=oT[:, dvc, sl], scalar=vec[:, 76 + dvc:77 + dvc], in1=rt[:],
                                op0=ALU.mult, op1=ALU.mult), reads=[R["FA"], rtR, constR], writes=[R["FA"]])
                            S.op("dve", lambda: nc.vector.tensor_tensor(out=ypc[:, dvc, sl], in0=oT[:, dvc, sl],
                                                                        in1=sgT[:, dvc, sl], op=ALU.mult),
                                 reads=[R["FA"], R["FB"]], writes=[R["ypc"]])
                    return outproj_gen(4 + h, Lag(mix_banks) if h == 0 else None)

                PU = [FA[:, 0:1040], FB[:, 1040:2080]]
                PA = FA[:, 1040:2080]
                PB = FB[:, 0:1040]
                RU = [Res(), Res()]
                RPA, RPB = Res(), Res()

                def pool_sync():
                    S.op("dve", lambda: nc.vector.memset(blast[:, 0:1], 0.0),
                         writes=[R["FA"], R["FB"], RU[0], RU[1], RPA, RPB, R["blast"]])

                def pool_ip(c):
                    U, UR = PU[c % 2], RU[c % 2]

                    def ev_u(t, sl, ps, psR):
                        S.op("act", lambda: nc.scalar.copy(out=U[:, 16 + t * 512:16 + (t + 1) * 512], in_=ps[:]),
                             reads=[psR], writes=[UR])
                    wt, wR = load_w(c)
                    proj_fm(wt, wR, ev_u)
                    ps, psR = nps()

                    def grp():
                        last = None
                        for kc in range(16):
                            last = nc.tensor.matmul(ps[:, 0:16], wt[:, kc * 128:(kc + 1) * 128],
                                                    halo_in[:, kc * 16:(kc + 1) * 16],
                                                    start=(kc == 0), stop=(kc == 15))
                        return last

                    S.op("pe", grp, reads=[wR, R["halo_in"]], writes=[psR])
                    S.op("act", lambda: nc.scalar.copy(out=U[:, 0:16], in_=ps[:, 0:16]), reads=[psR], writes=[UR])

                def pool_math(c):
                    gi, c2 = c // 2, c % 2
                    w = (2, 4, 8, 16)[gi]
                    U, UR = PU[c % 2], RU[c % 2]
                    pooled = sqb
                    cur, curR = U, UR
                    s_ = 1
                    for stp in range(gi + 1):
                        dstT, dstR = (PA, RPA) if (stp % 2 == 0) else (PB, RPB)
                        lo = 2 * s_ - 1
                        S.op("dve", lambda: nc.vector.tensor_tensor(out=dstT[:, lo:1040], in0=cur[:, lo:1040],
                                                                    in1=cur[:, lo - s_:1040 - s_], op=ALU.add),
                             reads=[curR], writes=[dstR])
                        cur, curR = dstT, dstR
                        s_ *= 2
                    S.op("dve", lambda: nc.vector.scalar_tensor_tensor(
                        out=pooled[:, c2, :], in0=cur[:, 16:1040], scalar=1.0 / w, in1=U[:, 16:1040],
                        op0=ALU.mult, op1=ALU.subtract), reads=[curR, UR], writes=[R["sqb"]])
                    S.op("dve", lambda: nc.vector.tensor_tensor(out=blast[:, 0:16], in0=cur[:, 16:32],
                                                                in1=cv[:, 8 + gi * 16:8 + (gi + 1) * 16],
                                                                op=ALU.mult),
                         reads=[curR, constR], writes=[R["blast"]])
                    S.op("dve", lambda: nc.vector.tensor_tensor(out=pooled[:, c2, 0:16], in0=blast[:, 0:16],
                                                                in1=U[:, 16:32], op=ALU.subtract),
                         reads=[R["blast"], UR], writes=[R["sqb"]])

                def pool_lin_out(gi, hook=None):
                    pooled = sqb
                    wp, wpR = wplrot.nxt()
                    S.op("pool", lambda: nc.gpsimd.dma_start(
                        out=wp[:], in_=wpl[gi * 256:(gi + 1) * 256, :].rearrange("(c p) d -> p c d", p=128)),
                        writes=[wpR], dma=wplrot.sem())
                    for dch in range(2):
                        for t in range(2):
                            sl = slice(t * 512, (t + 1) * 512)
                            ps, psR = nps()

                            def grp():
                                nc.tensor.matmul(ps[:], wp[:, 0, dch * 128:(dch + 1) * 128], pooled[:, 0, sl],
                                                 start=True, stop=False)
                                return nc.tensor.matmul(ps[:], wp[:, 1, dch * 128:(dch + 1) * 128], pooled[:, 1, sl],
                                                        start=False, stop=True)

                            S.op("pe", grp, reads=[wpR, R["sqb"]], writes=[psR])
                            col = 64 + gi * 2 + dch
                            S.op("act", lambda: nc.scalar.activation(out=ypc[:, dch, sl], in_=ps[:], func=AF.Copy,
                                                                     scale=vec[:, col:col + 1]),
                                 reads=[psR, constR], writes=[R["ypc"]])
                    outproj(gi, hook)

                def gather_sel(src_d, srcR, width, dst, dstR, rot):
                    for i, r in enumerate(range(0, NCORES, 2)):
                        if i > 0:
                            yield
                        stg, stgR = rot.nxt()
                        S.op("sp", lambda: nc.sync.dma_start(out=stg[:, 0:width], in_=src_d[r * 128:(r + 1) * 128, :]),
                             reads=[srcR], writes=[stgR], dma=rot.sem())
                        if i == 0:
                            S.op("dve", lambda: nc.vector.tensor_scalar(out=dst[:], in0=stg[:, 0:width],
                                                                        scalar1=cv[:, r:r + 1], scalar2=None,
                                                                        op0=ALU.mult),
                                 reads=[stgR, constR], writes=[dstR])
                        else:
                            S.op("dve", lambda: nc.vector.scalar_tensor_tensor(
                                out=dst[:], in0=stg[:, 0:width], scalar=cv[:, r:r + 1], in1=dst[:],
                                op0=ALU.mult, op1=ALU.add), reads=[stgR, dstR, constR], writes=[dstR])

                head_decay(0)
                vproj(0)
                interleave(head_qk_gen(0), None)
                for h in range(4):
                    if h < 3:
                        vproj(h + 1)
                    head_small(h)
                    if h < 3:
                        head_decay(h + 1)
                    interleave(head_state_gen(h), head_qk_gen(h + 1) if h < 3 else None)
                    if h == 1:
                        for _ in gather_sel(couth, R["couth"], 256, halo_in, R["halo_in"], stghrot):
                            pass
                S.op("pool", lambda: nc.gpsimd.collective_compute(
                    "AllGather", mybir.AluOpType.bypass, replica_groups=[list(range(NCORES))],
                    ins=[cin.ap().opt()], outs=[cout.ap().opt()]), reads=[R["cin"]], writes=[R["cout"]],
                    dma=d_cc, cc=True)
                pool_sync()
                stgrot = Rot([PB[:, 0:1024]], d_st)
                stgrot.R[0] = RPB
                gs = gather_sel(cout, R["cout"], 1024, carry, R["carry"], stgrot)
                pool_ip(0)
                for c in range(8):
                    if c < 7:
                        pool_ip(c + 1)
                    pool_math(c)
                    if c % 2 == 1:
                        pool_lin_out(c // 2, hook=(lambda: drain(gs, 1)) if c == 7 else None)
                drain(gs, 8)
                pool_sync()
                pending = None
                for h in (3, 1, 2, 0):
                    pending = head_back(h, pending)
                npsmod[0] = 5
                drain(pending, 8)
                S.barrier()

        def final():
            orot = Rot([sb(f"ot{i}", [128, 512], F32) for i in range(3)], d_out)

            def cb(j, t, sl, rt, rtR):
                ot, otR = orot.nxt()
                S.op("dve", lambda: nc.vector.scalar_tensor_tensor(
                    out=ot[:], in0=hT[:, j, sl], scalar=vec[:, 48 + j:49 + j], in1=rt[:],
                    op0=ALU.mult, op1=ALU.mult), reads=[hTR[j], rtR, constR], writes=[otR])
                S.op("sp", lambda: nc.sync.dma_start(out=outT[j * 128:(j + 1) * 128, sl], in_=ot[:]),
                     reads=[otR], dma=orot.sem())
            rmsnorm(cb, tail_banks if STAGES >= 3 else None)
            for dn in d_out:
                nc.sync.wait_ge(S.sem[dn], S.cnt[dn])

        d_out = [S.dsem(f"d_out{i}") for i in range(3)]
        stages = STAGES
        ffn("f1", w1i, w1o, 0)
        if stages >= 2:
            mixer()
        if stages >= 3:
            ffn("f2", w2i, w2o, 32, mix_banks)
        final()
    return nc


STAGES = 3


def _prep_shared(inp):
    f = np.float32
    def tile_win(w):
        return np.ascontiguousarray(w.reshape(16, 128, 2, NFC, 128).transpose(3, 1, 2, 0, 4)).reshape(NFC, 128, 4096)
    wmix = inp["w_in_mix"][0]
    sh = {
        "w1i": tile_win(inp["ffn1_w_in"][0]),
        "w1o": np.ascontiguousarray(inp["ffn1_w_out"][0], dtype=f),
        "w2i": tile_win(inp["ffn2_w_in"][0]),
        "w2o": np.ascontiguousarray(inp["ffn2_w_out"][0], dtype=f),
        "wmx": np.ascontiguousarray(wmix[:, :4096].reshape(16, 128, 32, 128).transpose(2, 1, 0, 3)).reshape(32, 128, 2048),
        "wmr": np.ascontiguousarray(wmix[:, 4096:4112].reshape(16, 128, 16).transpose(1, 0, 2)).reshape(128, 256),
        "wom": np.ascontiguousarray(inp["w_out_mix"][0], dtype=f),
        "wpl": np.ascontiguousarray(inp["w_pool"][0].reshape(1024, 256), dtype=f),
        "wal": np.ascontiguousarray(inp["w_alpha"][0], dtype=f),
    }
    def fm(v, n):
        return np.asarray(v, dtype=f).reshape(n, 128).T
    vecs = np.zeros((128, 80), f)
    vecs[:, 0:16] = fm(inp["ffn1_norm"][0], 16)
    vecs[:, 16:32] = fm(inp["mix_norm"][0], 16)
    vecs[:, 32:48] = fm(inp["ffn2_norm"][0], 16)
    vecs[:, 48:64] = fm(inp["final_norm"], 16)
    vecs[:, 64:72] = fm(inp["pool_scale"][0], 8)
    vecs[:, 72:76] = fm(inp["b_alpha"][0], 4)
    vecs[:, 76:78] = fm(inp["gla_norm"][0], 2)
    sh["vecs"] = vecs
    cst = np.zeros((128, 256), f)
    cst[:, 0:128] = np.eye(128, dtype=f)
    j = np.arange(128)[:, None]
    i = np.arange(128)[None, :]
    cst[:, 128:256] = ((j // 64 == i // 64) & (j <= i)).astype(f)
    sh["cst"] = cst
    return sh


def kernel(**inp):
    x = np.asarray(inp["x"], dtype=np.float32)
    B = x.shape[0]
    sh = _prep_shared({k: np.asarray(v) for k, v in inp.items()})
    in_maps = []
    for core in range(NCORES):
        b, half = core // 2, core % 2
        m = dict(sh)
        m["xT"] = np.ascontiguousarray(x[b, half * T:(half + 1) * T, :].T)
        cvec = np.zeros((128, 72), np.float32)
        if half == 1:
            cvec[:, core - 1] = 1.0
        for gi, w in enumerate((2, 4, 8, 16)):
            if half == 0:
                cnt = np.minimum(np.arange(16) + 1, w)
            else:
                cnt = np.full(16, w)
            cvec[:, 8 + gi * 16:8 + (gi + 1) * 16] = (1.0 / cnt.astype(np.float64)).astype(np.float32)[None, :]
        m["cvec"] = cvec
        in_maps.append(m)
    nc = build_nc()
    res = run_bass_kernel_spmd(nc, in_maps, core_ids=list(range(NCORES)))
    out = np.empty((B, 2 * T, D), np.float32)
    for core in range(NCORES):
        b, half = core // 2, core % 2
        out[b, half * T:(half + 1) * T, :] = res.results[core]["outT"].T
    return out
```
